# Optimizing a Trainium2 kernel written in Bass

```python
import math
import jax
import jax.numpy as jnp
from jax import lax
import numpy as np

D_MODEL = 1024
BATCH = 2
SEQ = 8192
DEPTH = 2
DEC_BATCH = 32
DEC_SEQ = 8
PAST_LEN = 16384
PAGE_SIZE = 128

N_EVEN = (DEPTH + 1) // 2
N_ODD = DEPTH // 2
RMS_EPS = 1e-6
D_FF = 4 * D_MODEL
GLA_HEADS = 4
GLA_DK = D_MODEL // 16
GLA_DV = D_MODEL // 8
GLA_GATE_RANK = 16
GLA_GATE_TEMP = 16.0
GLA_CHUNK = 64
SB_HEADS = 4
SB_HEAD_DIM = D_MODEL // 8
SB_Q_BLOCK = 128
MIX_WIDTH = GLA_HEADS * GLA_DV + SB_HEADS * SB_HEAD_DIM
EVEN_SPLIT_SIZES = (GLA_HEADS * GLA_DK, GLA_HEADS * GLA_DK, GLA_HEADS * GLA_DV, GLA_GATE_RANK,
                    GLA_HEADS * GLA_DV, SB_HEADS * SB_HEAD_DIM, SB_HEADS * SB_HEAD_DIM,
                    SB_HEADS * SB_HEAD_DIM)
EVEN_IN = sum(EVEN_SPLIT_SIZES)
SSD_INNER = 2 * D_MODEL
SSD_HEAD_DIM = 64
SSD_HEADS = SSD_INNER // SSD_HEAD_DIM
SSD_GROUPS = 8
SSD_STATE = 128
SSD_CONV = 4
SSD_CHUNK = 128
SSD_CONV_DIM = SSD_INNER + 2 * SSD_GROUPS * SSD_STATE
SSD_IN = SSD_INNER + SSD_CONV_DIM + SSD_HEADS

kernel_name = 'hybrid_gla_stickbreak_ssd_step'


def _rms(x):
    x32 = x.astype(jnp.float32)
    return x32 * lax.rsqrt(jnp.mean(jnp.square(x32), axis=-1, keepdims=True) + RMS_EPS)


def rms_norm(x, g):
    return (_rms(x) * g.astype(jnp.float32)).astype(x.dtype)


def _pad_seq(t, pad):
    if pad == 0:
        return t
    return jnp.pad(t, [(0, 0), (0, pad)] + [(0, 0)] * (t.ndim - 2))


def _to_chunks(t, n, c):
    return jnp.moveaxis(t.reshape((t.shape[0], n, c) + t.shape[2:]), 1, 0)


def _from_chunks(t, length):
    t = jnp.moveaxis(t, 0, 1)
    return t.reshape((t.shape[0], t.shape[1] * t.shape[2]) + t.shape[3:])[:, :length]


def gla_chunked(q, k, v, log_a, s0):
    length = q.shape[1]
    c = min(GLA_CHUNK, length)
    n = -(-length // c)
    pad = n * c - length
    q, k, v, log_a = (_to_chunks(_pad_seq(t, pad), n, c) for t in (q, k, v, log_a))
    tri = jnp.tril(jnp.ones((c, c), dtype=bool))[None, :, :, None, None]

    def step(s, inp):
        qc, kc, vc, ac = inp
        b = jnp.cumsum(ac, axis=1)
        decay = jnp.exp(jnp.where(tri, b[:, :, None] - b[:, None], -jnp.inf))
        att = jnp.einsum('bthd,bshd,btshd->bhts', qc, kc, decay)
        o = (jnp.einsum('bhts,bshe->bthe', att, vc)
             + jnp.einsum('bthd,bhde->bthe', qc * jnp.exp(b), s))
        last = b[:, -1]
        s_new = (jnp.exp(last)[..., None] * s
                 + jnp.einsum('bshd,bshe->bhde', kc * jnp.exp(last[:, None] - b), vc))
        return s_new, o

    s_fin, o = lax.scan(step, s0.astype(jnp.float32), (q, k, v, log_a))
    return _from_chunks(o, length), s_fin


def sb_attend(q, k, v, logit_bias, q_start):
    bsz, lq, nh, hd = q.shape
    scale = hd ** -0.5
    kpos = jnp.arange(k.shape[1])
    bias = logit_bias.astype(jnp.float32)[None, :, None, None]

    def block(args):
        qb, start = args
        z = jnp.einsum('bqhd,bkhd->bhqk', qb, k).astype(jnp.float32) * scale + bias
        qpos = start + jnp.arange(qb.shape[1])
        mask = kpos[None, :] < qpos[:, None]
        log_1mb = jnp.where(mask, jax.nn.log_sigmoid(-z), 0.0)
        suffix = lax.cumsum(log_1mb, axis=3, reverse=True) - log_1mb
        a = jnp.where(mask, jnp.exp(jax.nn.log_sigmoid(z) + suffix), 0.0)
        return jnp.einsum('bhqk,bkhd->bqhd', a.astype(v.dtype), v)

    if lq > SB_Q_BLOCK and lq % SB_Q_BLOCK == 0:
        nb = lq // SB_Q_BLOCK
        qb = jnp.swapaxes(q.reshape(bsz, nb, SB_Q_BLOCK, nh, hd), 0, 1)
        starts = q_start + jnp.arange(nb, dtype=jnp.int32) * SB_Q_BLOCK
        o = lax.map(block, (qb, starts))
        return jnp.swapaxes(o, 0, 1).reshape(bsz, lq, nh, hd)
    return block((q, jnp.asarray(q_start, jnp.int32)))


def even_mixer(h, past_k, past_v, gla_s0, w_in, w_gate2, b_gate, gla_g, sb_bias, w_out):
    bsz, length, _ = h.shape
    split_at = [int(i) for i in np.cumsum(EVEN_SPLIT_SIZES)[:-1]]
    gq, gk, gv, glr, gr, sq, sk, sv = jnp.split(h @ w_in, split_at, axis=-1)

    def heads(t, nh):
        return t.reshape(bsz, length, nh, -1)

    log_a = jax.nn.log_sigmoid((glr @ w_gate2 + b_gate).astype(jnp.float32)) / GLA_GATE_TEMP
    o_gla, s_new = gla_chunked(heads(gq, GLA_HEADS) * GLA_DK ** -0.5, heads(gk, GLA_HEADS),
                               heads(gv, GLA_HEADS), heads(log_a, GLA_HEADS), gla_s0)
    o_gla = ((_rms(o_gla) * gla_g.astype(jnp.float32)).astype(h.dtype)
             * jax.nn.silu(heads(gr, GLA_HEADS)))
    sk, sv = heads(sk, SB_HEADS), heads(sv, SB_HEADS)
    k_all = jnp.concatenate([past_k.astype(sk.dtype), sk], axis=1)
    v_all = jnp.concatenate([past_v.astype(sv.dtype), sv], axis=1)
    o_sb = sb_attend(heads(sq, SB_HEADS), k_all, v_all, sb_bias, k_all.shape[1] - length)
    mixed = jnp.concatenate([o_gla.reshape(bsz, length, -1),
                             o_sb.reshape(bsz, length, -1)], axis=-1)
    return mixed @ w_out, sk, sv, s_new.astype(gla_s0.dtype)


def ssd_chunked(x, dt, a, bm, cm, h0):
    bsz, length, nh, hp = x.shape
    ng, ns = bm.shape[2], bm.shape[3]
    hpg = nh // ng
    c = min(SSD_CHUNK, length)
    n = -(-length // c)
    pad = n * c - length
    x, dt, bm, cm = (_to_chunks(_pad_seq(t, pad), n, c) for t in (x, dt, bm, cm))
    tri = jnp.tril(jnp.ones((c, c), dtype=bool))[None, :, :, None]

    def step(h, inp):
        xc, dtc, bc, cc = inp
        cum = jnp.cumsum(dtc * a, axis=1)
        w = jnp.exp(jnp.where(tri, cum[:, :, None] - cum[:, None], -jnp.inf)) * dtc[:, None]
        cb = jnp.einsum('btgn,bsgn->btsg', cc, bc)
        w = w.reshape(bsz, c, c, ng, hpg) * cb[..., None]
        xg = xc.reshape(bsz, c, ng, hpg, hp)
        hg = h.reshape(bsz, ng, hpg, hp, ns)
        y = (jnp.einsum('btsgj,bsgjp->btgjp', w, xg)
             + jnp.einsum('btgn,bgjpn->btgjp', cc, hg)
             * jnp.exp(cum).reshape(bsz, c, ng, hpg)[..., None])
        last = cum[:, -1]
        dec = (jnp.exp(last[:, None] - cum) * dtc).reshape(bsz, c, ng, hpg)
        h_new = (jnp.exp(last)[..., None, None] * h
                 + jnp.einsum('bsgn,bsgjp->bgjpn', bc, xg * dec[..., None]).reshape(bsz, nh, hp, ns))
        return h_new, y.reshape(bsz, c, nh, hp)

    h_fin, ys = lax.scan(step, h0.astype(jnp.float32), (x, dt, bm, cm))
    return _from_chunks(ys, length), h_fin


def odd_mixer(h, conv0, h0, w_in, conv_w, conv_b, dt_bias, a_log, d_skip, norm_g, w_out):
    bsz, length, _ = h.shape
    z, xbc, dt_raw = jnp.split(h @ w_in, [SSD_INNER, SSD_INNER + SSD_CONV_DIM], axis=-1)
    xbc_ext = jnp.concatenate([conv0.astype(xbc.dtype), xbc], axis=1)
    conv_new = xbc_ext[:, length:]
    xbc = jax.nn.silu(sum(xbc_ext[:, w:w + length] * conv_w[w] for w in range(SSD_CONV)) + conv_b)
    xs, bm, cm = jnp.split(xbc, [SSD_INNER, SSD_INNER + SSD_GROUPS * SSD_STATE], axis=-1)
    xs = xs.reshape(bsz, length, SSD_HEADS, SSD_HEAD_DIM)
    bm = bm.reshape(bsz, length, SSD_GROUPS, SSD_STATE)
    cm = cm.reshape(bsz, length, SSD_GROUPS, SSD_STATE)
    dt = jax.nn.softplus(dt_raw.astype(jnp.float32) + dt_bias.astype(jnp.float32))
    a = -jnp.exp(a_log.astype(jnp.float32))
    y, h_new = ssd_chunked(xs, dt, a, bm, cm, h0)
    y = (y + xs * d_skip[:, None]).reshape(bsz, length, SSD_INNER) * jax.nn.silu(z)
    y = (_rms(y.reshape(bsz, length, SSD_GROUPS, -1)).reshape(bsz, length, SSD_INNER)
         * norm_g.astype(jnp.float32))
    return y.astype(h.dtype) @ w_out, conv_new, h_new.astype(h0.dtype)


def sq_relu_mlp(h, w_up, w_down):
    return jnp.square(jax.nn.relu(h @ w_up)) @ w_down


def gather_pages(pool, page_table):
    g = jnp.take(pool, page_table, axis=0)
    return g.reshape((g.shape[0], g.shape[1] * g.shape[2]) + g.shape[3:])


def run_trunk(x, sb_past_k, sb_past_v, gla_s0, ssm_h0, conv0, p):
    sb_k_new, sb_v_new, gla_new, ssm_new, conv_new = [], [], [], [], []
    for li in range(DEPTH):
        hn = rms_norm(x, p['norm_mix_pre'][li])
        if li % 2 == 0:
            e = li // 2
            m, k_n, v_n, s_n = even_mixer(hn, sb_past_k[e], sb_past_v[e], gla_s0[e],
                                          p['w_in_even'][e], p['gla_w_gate2'][e],
                                          p['gla_b_gate'][e], p['gla_norm_g'][e],
                                          p['sb_logit_bias'][e], p['w_out_even'][e])
            sb_k_new.append(k_n)
            sb_v_new.append(v_n)
            gla_new.append(s_n)
        else:
            o = li // 2
            m, c_n, h_n = odd_mixer(hn, conv0[o], ssm_h0[o], p['ssd_w_in'][o], p['ssd_conv_w'][o],
                                    p['ssd_conv_b'][o], p['ssd_dt_bias'][o], p['ssd_a_log'][o],
                                    p['ssd_d'][o], p['ssd_norm_g'][o], p['ssd_w_out'][o])
            conv_new.append(c_n)
            ssm_new.append(h_n)
        x = x + rms_norm(m, p['norm_mix_post'][li])
        f = sq_relu_mlp(rms_norm(x, p['norm_mlp_pre'][li]), p['mlp_w_up'][li], p['mlp_w_down'][li])
        x = x + rms_norm(f, p['norm_mlp_post'][li])
    return x, (jnp.stack(sb_k_new), jnp.stack(sb_v_new), jnp.stack(gla_new),
               jnp.stack(ssm_new), jnp.stack(conv_new))


def setup_inputs(seed: int = 0) -> dict:
    key = jax.random.key(seed)
    ks = list(jax.random.split(key, 40))

    def nrm(shape, scale):
        return jax.random.normal(ks.pop(), shape, jnp.float32) * scale

    n_pages = PAST_LEN // PAGE_SIZE
    n_pool = (DEC_BATCH * n_pages * 5 + 3) // 4
    page_table = (jax.random.permutation(ks.pop(), n_pool)[: DEC_BATCH * n_pages]
                  .reshape(DEC_BATCH, n_pages).astype(jnp.int32))
    dt0 = jnp.exp(jax.random.uniform(ks.pop(), (N_ODD, SSD_HEADS), jnp.float32,
                                     minval=math.log(1e-3), maxval=math.log(1e-1)))
    ssd_dt_bias = dt0 + jnp.log(-jnp.expm1(-dt0))
    ssd_a_log = jnp.log(jax.random.uniform(ks.pop(), (N_ODD, SSD_HEADS), jnp.float32,
                                           minval=1.0, maxval=16.0))
    sb_logit_bias = jax.random.uniform(ks.pop(), (N_EVEN, SB_HEADS), jnp.float32,
                                       minval=-9.0, maxval=-6.0)
    return {
        'x_prompt': nrm((BATCH, SEQ, D_MODEL), 1.0),
        'x_sample': nrm((DEC_BATCH, DEC_SEQ, D_MODEL), 1.0),
        'cache_sb_k': nrm((N_EVEN, n_pool, PAGE_SIZE, SB_HEADS, SB_HEAD_DIM), 1.0),
        'cache_sb_v': nrm((N_EVEN, n_pool, PAGE_SIZE, SB_HEADS, SB_HEAD_DIM), 1.0),
        'state_gla': nrm((N_EVEN, DEC_BATCH, GLA_HEADS, GLA_DK, GLA_DV), 0.5),
        'state_ssm': nrm((N_ODD, DEC_BATCH, SSD_HEADS, SSD_HEAD_DIM, SSD_STATE), 0.1),
        'state_conv': nrm((N_ODD, DEC_BATCH, SSD_CONV - 1, SSD_CONV_DIM), 1.0),
        'page_table': page_table,
        'w_in_even': nrm((N_EVEN, D_MODEL, EVEN_IN), D_MODEL ** -0.5),
        'gla_w_gate2': nrm((N_EVEN, GLA_GATE_RANK, GLA_HEADS * GLA_DK), GLA_GATE_RANK ** -0.5),
        'gla_b_gate': nrm((N_EVEN, GLA_HEADS * GLA_DK), 0.1),
        'gla_norm_g': 1.0 + nrm((N_EVEN, GLA_DV), 0.01),
        'sb_logit_bias': sb_logit_bias,
        'w_out_even': nrm((N_EVEN, MIX_WIDTH, D_MODEL), MIX_WIDTH ** -0.5),
        'ssd_w_in': nrm((N_ODD, D_MODEL, SSD_IN), D_MODEL ** -0.5),
        'ssd_conv_w': nrm((N_ODD, SSD_CONV, SSD_CONV_DIM), SSD_CONV ** -0.5),
        'ssd_conv_b': nrm((N_ODD, SSD_CONV_DIM), 0.01),
        'ssd_dt_bias': ssd_dt_bias,
        'ssd_a_log': ssd_a_log,
        'ssd_d': 1.0 + nrm((N_ODD, SSD_HEADS), 0.01),
        'ssd_norm_g': 1.0 + nrm((N_ODD, SSD_INNER), 0.01),
        'ssd_w_out': nrm((N_ODD, SSD_INNER, D_MODEL), SSD_INNER ** -0.5),
        'norm_mix_pre': 1.0 + nrm((DEPTH, D_MODEL), 0.01),
        'norm_mix_post': 1.0 + nrm((DEPTH, D_MODEL), 0.01),
        'norm_mlp_pre': 1.0 + nrm((DEPTH, D_MODEL), 0.01),
        'norm_mlp_post': 1.0 + nrm((DEPTH, D_MODEL), 0.01),
        'mlp_w_up': nrm((DEPTH, D_MODEL, D_FF), D_MODEL ** -0.5),
        'mlp_w_down': nrm((DEPTH, D_FF, D_MODEL), D_FF ** -0.5),
    }


def reference(x_prompt, x_sample, cache_sb_k, cache_sb_v, state_gla, state_ssm, state_conv,
              page_table, w_in_even, gla_w_gate2, gla_b_gate, gla_norm_g, sb_logit_bias,
              w_out_even, ssd_w_in, ssd_conv_w, ssd_conv_b, ssd_dt_bias, ssd_a_log, ssd_d,
              ssd_norm_g, ssd_w_out, norm_mix_pre, norm_mix_post, norm_mlp_pre, norm_mlp_post,
              mlp_w_up, mlp_w_down):
    params = dict(w_in_even=w_in_even, gla_w_gate2=gla_w_gate2, gla_b_gate=gla_b_gate,
                  gla_norm_g=gla_norm_g, sb_logit_bias=sb_logit_bias, w_out_even=w_out_even,
                  ssd_w_in=ssd_w_in, ssd_conv_w=ssd_conv_w, ssd_conv_b=ssd_conv_b,
                  ssd_dt_bias=ssd_dt_bias, ssd_a_log=ssd_a_log, ssd_d=ssd_d,
                  ssd_norm_g=ssd_norm_g, ssd_w_out=ssd_w_out,
                  norm_mix_pre=norm_mix_pre, norm_mix_post=norm_mix_post,
                  norm_mlp_pre=norm_mlp_pre, norm_mlp_post=norm_mlp_post,
                  mlp_w_up=mlp_w_up, mlp_w_down=mlp_w_down)
    bp = x_prompt.shape[0]
    dtp = x_prompt.dtype
    empty = jnp.zeros((bp, 0, SB_HEADS, SB_HEAD_DIM), dtp)
    y_prompt, (sb_k_p, sb_v_p, gla_p, ssm_p, conv_p) = run_trunk(
        x_prompt, [empty] * N_EVEN, [empty] * N_EVEN,
        jnp.zeros((N_EVEN, bp, GLA_HEADS, GLA_DK, GLA_DV), dtp),
        jnp.zeros((N_ODD, bp, SSD_HEADS, SSD_HEAD_DIM, SSD_STATE), dtp),
        jnp.zeros((N_ODD, bp, SSD_CONV - 1, SSD_CONV_DIM), dtp), params)
    past_k = [gather_pages(cache_sb_k[e], page_table) for e in range(N_EVEN)]
    past_v = [gather_pages(cache_sb_v[e], page_table) for e in range(N_EVEN)]
    y_sample, (sb_k_s, sb_v_s, gla_s, ssm_s, conv_s) = run_trunk(
        x_sample, past_k, past_v, state_gla, state_ssm, state_conv, params)
    return (y_prompt, y_sample, sb_k_p, sb_v_p, gla_p, ssm_p, conv_p,
            sb_k_s, sb_v_s, gla_s, ssm_s, conv_s)
```

```python
import numpy as np
import concourse.bass as bass
import concourse.mybir as mybir
from concourse.bass_utils import run_bass_kernel_spmd

F32 = mybir.dt.float32
BF = mybir.dt.bfloat16
I32 = mybir.dt.int32
AF = mybir.ActivationFunctionType
AL = mybir.AluOpType

NT = 2080
TILES = [(0, 512), (512, 512), (1024, 512), (1536, 512), (2048, 32)]
STS = [[0], [1], [2], [3, 4]]
GROUPS = [[0, 1, 2, 3], [4, 5, 6, 7]]
EPS = 1e-6
SB_SCALE = 128.0 ** -0.5
NDMA = 40
import os
ENV = os.environ
USED = {'xT', 'rank', 'norms', 'cst', 'w0', 'w2g', 'bg', 'gng', 'sbb', 'msk', 'pt', 'iota', 'sgla', 'ridx', 'wout0', 'wup', 'wdn', 'w1', 'wout1', 'convw', 'convb', 'dtb', 'alog', 'dsk', 'sng', 'sssm', 'sconv'} | (set() if ENV.get('NOSAMPLE') else {'pkv'})


class Prog:
    def __init__(self, nc, sems):
        self.nc = nc
        self.sems = sems
        self.cnt = {e: 0 for e in ("act", "pool", "pe", "dve")}
        self.seen = {e: {} for e in ("sp", "act", "pool", "pe", "dve")}
        self.lastw = {}
        self.rd = {}
        self.ops = {e: [] for e in ("sp", "act", "pool", "pe", "dve")}
        self.dma_i = 0
        self.dma_val = [0] * NDMA
        self.cc_n = 0

    def _deps(self, reads, writes):
        toks = []
        for b in reads:
            t = self.lastw.get(b)
            if t:
                toks.append(t)
        for b in writes:
            t = self.lastw.get(b)
            if t:
                toks.append(t)
            toks += list(self.rd.get(b, {}).items())
        return toks

    def _waits(self, eng, toks):
        w = {}
        for key, val in toks:
            if key == ("e", "pe") and eng == "pe":
                continue
            if self.seen[eng].get(key, 0) >= val:
                continue
            w[key] = max(w.get(key, 0), val)
        for k, v in w.items():
            self.seen[eng][k] = v
        return list(w.items())

    def _commit(self, tok, reads, writes):
        for b in writes:
            self.lastw[b] = tok
            self.rd[b] = {}
        for b in reads:
            d = self.rd.setdefault(b, {})
            d[tok[0]] = max(d.get(tok[0], 0), tok[1])

    @staticmethod
    def _excl(reads, writes):
        r2 = [b for b in reads if not (isinstance(b, str) and b.startswith("ps"))]
        w2 = list(writes) + [b for b in reads if isinstance(b, str) and b.startswith("ps")]
        return r2, w2

    defer = None

    def begin(self):
        self.defer = []

    def end(self):
        lst, self.defer = self.defer, None
        return lst

    def after(self, tag):
        self.defer.append(("after", tag))

    def run_streams(self, streams, depth):
        alltags = {t for t, _ in streams}
        pending = [[t, l, 0] for t, l in streams]
        active, done = [], set()
        while pending or active:
            while len(active) < depth and pending:
                active.append(pending.pop(0))
            progressed = False
            for st in list(active):
                tag, lst, idx = st
                if idx >= len(lst):
                    active.remove(st)
                    done.add(tag)
                    progressed = True
                    continue
                item = lst[idx]
                if item[0] == "after":
                    if item[1] in done or item[1] not in alltags:
                        st[2] += 1
                        progressed = True
                    continue
                kind, eng, fn, reads, writes = item
                getattr(self, kind)(eng, fn, reads, writes) if kind != "cc" else self.cc(fn, reads, writes)
                st[2] += 1
                progressed = True
            assert progressed, "stream deadlock"

    def op(self, eng, fn, reads=(), writes=()):
        if self.defer is not None:
            self.defer.append(("op", eng, fn, list(reads), list(writes)))
            return
        reads, writes = self._excl(reads, writes)
        waits = self._waits(eng, self._deps(reads, writes))
        self.cnt[eng] += 1
        tok = (("e", eng), self.cnt[eng])
        self.ops[eng].append((waits, fn, tok[0], 1))
        self._commit(tok, reads, writes)

    def dma(self, eng, fn, reads=(), writes=()):
        if self.defer is not None:
            self.defer.append(("dma", eng, fn, list(reads), list(writes)))
            return
        waits = self._waits(eng, self._deps(reads, writes))
        i = self.dma_i % NDMA
        self.dma_i += 1
        self.dma_val[i] += 16
        tok = (("d", i), self.dma_val[i])
        self.ops[eng].append((waits, fn, tok[0], 16))
        self._commit(tok, reads, writes)

    def cc(self, fn, reads=(), writes=()):
        waits = self._waits("pool", self._deps(reads, writes))
        self.cc_n += 1
        tok = (("c", 0), self.cc_n)
        self.ops["pool"].append((waits, fn, tok[0], 1))
        self._commit(tok, reads, writes)

    def sem(self, key):
        if key[0] == "e":
            return self.sems[key[1]]
        if key[0] == "d":
            return self.sems["dma"][key[1]]
        return self.sems["cc"]

    def _run(self, name, e):
        for waits, fn, key, inc in self.ops[name]:
            for k, v in waits:
                e.wait_ge(self.sem(k), v)
            fn(e).then_inc(self.sem(key), inc)
        self.ops[name] = []

    def barrier(self):
        toks = [(("e", e), c) for e, c in self.cnt.items() if c]
        toks += [(("d", i), v) for i, v in enumerate(self.dma_val) if v]
        if self.cc_n:
            toks.append((("c", 0), self.cc_n))
        for eng in ("sp", "act", "pool", "pe", "dve"):
            waits = self._waits(eng, toks)
            if not waits:
                continue
            if eng in self.cnt:
                self.cnt[eng] += 1
                key = ("e", eng)
                self.ops[eng].append((waits, lambda e: e.nop(), key, 1))
            else:
                self.ops[eng].append((waits, None, None, 0))

    def emit(self):
        self.barrier()
        with self.nc.Block() as block:
            @block.sync
            def _(e):
                for waits, fn, key, inc in self.ops["sp"]:
                    for k, v in waits:
                        e.wait_ge(self.sem(k), v)
                    if fn is not None:
                        fn(e).then_inc(self.sem(key), inc)
                self.ops["sp"] = []

            @block.scalar
            def _(e):
                self._run("act", e)

            @block.gpsimd
            def _(e):
                self._run("pool", e)

            @block.tensor
            def _(e):
                self._run("pe", e)

            @block.vector
            def _(e):
                self._run("dve", e)


def act(P, out, in_, func, reads, writes, bias=0.0, scale=1.0):
    P.op("act", lambda e: e.activation(out, in_, func, bias=bias, scale=scale), reads, writes)


def mm(P, out, lhsT, rhs, start, stop, reads, writes):
    P.op("pe", lambda e: e.matmul(out, lhsT, rhs, start=start, stop=stop), reads, writes)


def tt(P, eng, out, in0, in1, op, reads, writes):
    P.op(eng, lambda e: e.tensor_tensor(out, in0, in1, op), reads, writes)


def stt(P, eng, out, in0, scalar, in1, op0, op1, reads, writes):
    P.op(eng, lambda e: e.scalar_tensor_tensor(out, in0, scalar, in1, op0, op1), reads, writes)


def ts(P, eng, out, in0, s1, s2, op0, op1, reads, writes):
    if s2 is None:
        P.op(eng, lambda e: e.tensor_scalar(out, in0, s1, None, op0), reads, writes)
    else:
        P.op(eng, lambda e: e.tensor_scalar(out, in0, s1, s2, op0, op1), reads, writes)


def cp(P, eng, out, in_, reads, writes):
    if eng == "act":
        P.op("act", lambda e: e.copy(out, in_), reads, writes)
    else:
        P.op(eng, lambda e: e.tensor_copy(out, in_), reads, writes)


def dma(P, eng, out, in_, reads, writes):
    P.dma(eng, lambda e: e.dma_start(out=out, in_=in_), reads, writes)


def build():
    nc = bass.Bass("TRN2", target_bir_lowering=False)
    D = {}

    def din(name, shape, dt=F32):
        if name not in USED:
            return
        D[name] = nc.dram_tensor(name, list(shape), dt, kind="ExternalInput").ap()

    def dout(name, shape, dt=F32):
        D[name] = nc.dram_tensor(name, list(shape), dt, kind="ExternalOutput").ap()

    def dint(name, shape, dt=BF):
        D[name] = nc.dram_tensor(name, list(shape), dt).ap()

    din("xT", [1024, NT])
    din("rank", [1, 4], I32)
    din("w0", [1024, 784])
    din("w2g", [16, 64])
    din("bg", [1, 64])
    din("gng", [128, 1])
    din("sbb", [128, 1])
    din("wout0", [1024, 1024])
    din("norms", [128, 8 * 8])
    din("wup", [2, 1024, 4096])
    din("wdn", [2, 4096, 1024])
    din("sgla", [16, 64, 128])
    din("pkv", [5120 * 128, 256])
    din("iota", [128, 1])
    din("ridx", [128, 24], I32)
    din("pt", [1, 2048], I32)
    din("cst", [128, 4 * 128])
    din("msk", [128, 5 * 512])
    din("w1", [1024, 1544])
    din("wout1", [2048, 1024])
    din("convw", [128, 8 * 4])
    din("convb", [128, 8])
    din("dtb", [1, 8])
    din("alog", [1, 8])
    din("dsk", [1, 512])
    din("sng", [1, 512])
    din("sssm", [16, 128, 512])
    din("sconv", [16, 128, 8 * 3])

    dout("yT", [1024, NT])
    dout("ko", [128, 8320])
    dout("vo", [8320, 128])
    dout("glao", [17, 64, 128])
    dout("ssmo", [17, 128, 512])
    dout("convo", [17, 128, 8 * 3])

    for c_ in range(8):
        dint("b_hn%d" % c_, [128, NT])
        dint("g_hn%d" % c_, [512, NT])
    dint("b_mx", [8 * 128, NT])
    dint("g_mx", [8 * 512, NT])
    dout("dbg", [8 * 128, NT], BF)
    dint("b_y1", [16 * 128, NT])
    dint("g_y1", [16 * 512, NT])

    sems = {}
    import contextlib
    with contextlib.ExitStack() as es:
        for e in ("act", "pool", "pe", "dve"):
            sems[e] = es.enter_context(nc.semaphore("s_" + e))
        sems["cc"] = es.enter_context(nc.semaphore("s_cc"))
        sems["dma"] = [es.enter_context(nc.semaphore("s_d%d" % i)) for i in range(NDMA)]
        P = Prog(nc, sems)

        def sb(name, shape, dt):
            return es.enter_context(nc.sbuf_tensor("t_" + name, list(shape), dt))

        def ps(name, shape, dt=F32):
            return es.enter_context(nc.psum_tensor(name, list(shape), dt))

        x_sb = sb("x_sb", [128, 8, NT], F32)
        nrm = sb("nrm", [128, 64], F32)
        cstf = sb("cstf", [128, 512], F32)
        cst = sb("cst", [128, 512], BF)
        rank_sb = sb("rank_sb", [1, 4], I32)
        tri, su, ones, ident = (cst[:, i * 128:(i + 1) * 128] for i in range(4))
        PS = [ps("ps%d" % i, [128, 512]) for i in range(8)]

        for c in range(8):
            dma(P, "sp", x_sb[:, c, :], D["xT"][c * 128:(c + 1) * 128, :], [], [("x", c)])
        dma(P, "sp", nrm[:], D["norms"], [], ["nrm"])
        dma(P, "pool", cst[:], D["cst"], [], ["cst"])
        dma(P, "sp", cstf[:], D["cst"], [], ["cstf"])
        dma(P, "sp", rank_sb[:], D["rank"], [], ["rank"])

        def rmsnorm_tile(ti, nidx, out_fn, out_keys, tmp):
            t0, w = TILES[ti]
            sq, rstd, pst, pkey = tmp
            for c in range(8):
                act(P, sq[:, c, :w], x_sb[:, c, t0:t0 + w], AF.Square, [("x", c)], ["sq"])
            for c in range(8):
                mm(P, pst[:, :w], ones, sq[:, c, :w], c == 0, c == 7, ["sq", "cst"], [pkey])
            act(P, rstd[:, :w], pst[:, :w], AF.Ln, [pkey], ["rstd"], bias=EPS, scale=1.0 / 1024)
            act(P, rstd[:, :w], rstd[:, :w], AF.Exp, ["rstd"], ["rstd"], scale=-0.5)
            for c in range(8):
                stt(P, "dve", out_fn(c), x_sb[:, c, t0:t0 + w], nrm[:, nidx * 8 + c:nidx * 8 + c + 1],
                    rstd[:, :w], AL.mult, AL.mult, [("x", c), "rstd", "nrm"], out_keys)

        def allgather(src, dst, skey, dkey):
            P.cc(lambda e: e.collective_compute("AllGather", AL.bypass, replica_groups=GROUPS,
                                                ins=[src], outs=[dst]), [skey], [dkey])

        def prenorm_gather(nidx, tag):
            with contextlib.ExitStack() as es1:
                sq = es1.enter_context(nc.sbuf_tensor(tag + "sq", [128, 8, 512], BF))
                rstd = es1.enter_context(nc.sbuf_tensor(tag + "rstd", [128, 512], F32))
                hT = [es1.enter_context(nc.sbuf_tensor(tag + "hT%d" % i, [128, 8, 512], BF)) for i in range(2)]
                for ti, (t0, w) in enumerate(TILES):
                    h = hT[ti % 2]
                    hk = ("hT", ti % 2)
                    rmsnorm_tile(ti, nidx, lambda c: h[:, c, :w], [hk], (sq, rstd, PS[0], "ps0"))
                    for c in range(8):
                        dma(P, "sp", D["b_hn%d" % c][:, t0:t0 + w], h[:, c, :w], [hk], ["b_hn%d" % c])
                for c in range(8):
                    allgather(D["b_hn%d" % c], D["g_hn%d" % c], "b_hn%d" % c, "g_hn%d" % c)
                P.emit()
        prenorm_gather(0, "p0")

        with contextlib.ExitStack() as es2:
            def sb2(name, shape, dt):
                return es2.enter_context(nc.sbuf_tensor("m_" + name, list(shape), dt))
            w0_sb = sb2("w0", [128, 8, 784], BF)
            w2_sb = sb2("w2", [16, 64], BF)
            bg_bc = sb2("bgbc", [128, 64], F32)
            gng = sb2("gng", [128, 1], F32)
            bias_sb = sb2("bias", [128, 1], F32)
            msk = sb2("msk", [128, 2560], BF)
            pt_sb = sb2("pt", [128, 2048], I32)
            ptf = sb2("ptf", [128, 2048], F32)
            iota_sb = sb2("iota", [128, 1], F32)
            e_sb = [sb2("e%d" % i, [128, 512], F32) for i in range(2)]
            sp_sb = [sb2("sp%d" % i, [128, 512], BF) for i in range(2)]
            w_sb = [sb2("wt%d" % i, [128, 512], F32) for i in range(2)]
            a_sb = [sb2("a%d" % i, [128, 512], BF) for i in range(2)]
            qTs = sb2("qTs", [128, 128], BF)
            kTs = sb2("kTs", [128, 128], BF)
            vs = sb2("vs", [8, 16, 128], BF)
            o_st = sb2("o_st", [128, 512], BF)
            dma(P, "pool", w0_sb[:], D["w0"].rearrange("(c p) n -> p c n", p=128), [], ["w0"])
            dma(P, "pool", w2_sb[:], D["w2g"], [], ["w2"])
            dma(P, "pool", msk[:], D["msk"], [], ["msk"])
            dma(P, "sp", bg_bc[:], D["bg"].partition_broadcast(128), [], ["bg"])
            dma(P, "sp", gng[:], D["gng"], [], ["gng"])
            dma(P, "sp", bias_sb[:], D["sbb"], [], ["bias"])
            dma(P, "sp", pt_sb[:], D["pt"].partition_broadcast(128), [], ["pt"])
            dma(P, "sp", iota_sb[:], D["iota"], [], ["iota"])
            cp(P, "dve", ptf[:], pt_sb[:], ["pt"], ["ptf"])
            ts(P, "dve", ptf[:], ptf[:], 128.0, iota_sb[:, 0:1], AL.mult, AL.add, ["ptf", "iota"], ["ptf"])
            cp(P, "dve", pt_sb[:], ptf[:], ["ptf"], ["pt"])

            def sb_step(si, first, last, nk, Tq, zmm, avmm, mask, sbank=4, after=None):
                par = si % 2
                psz, pszk = PS[2 + par], "ps%d" % (2 + par)
                pss, pssk, psc, psck = PS[sbank], "ps%d" % sbank, PS[5], "ps5"
                e_, sp_, w_, a_ = e_sb[par], sp_sb[par], w_sb[par], a_sb[par]
                ek, spk, wk, ak = ("e", par), ("sp", par), ("w", par), ("a", par)
                zmm(psz, pszk)
                act(P, e_[:nk, :Tq], psz[:nk, :Tq], AF.Exp, [pszk, "bias"], [ek], bias=bias_sb[:nk, 0:1], scale=SB_SCALE)
                act(P, sp_[:nk, :Tq], e_[:nk, :Tq], AF.Ln, [ek], [spk], bias=1.0)
                if mask is not None:
                    tt(P, "dve", sp_[:nk, :Tq], sp_[:nk, :Tq], mask, AL.mult, [spk, "msk"], [spk])
                mm(P, pss[:nk, :Tq], su[:nk, :nk], sp_[:nk, :Tq], True, True, [spk, "cst"], [pssk])
                stt(P, "dve", w_[:nk, :Tq], psz[:nk, :Tq], SB_SCALE, sp_[:nk, :Tq], AL.mult, AL.subtract,
                    [pszk, spk], [wk])
                tt(P, "dve", w_[:nk, :Tq], w_[:nk, :Tq], pss[:nk, :Tq], AL.subtract, [wk, pssk], [wk])
                if not first:
                    if after is not None:
                        P.after(after)
                    tt(P, "dve", w_[:nk, :Tq], w_[:nk, :Tq], psc[:nk, :Tq], AL.subtract, [wk, psck], [wk])
                act(P, a_[:nk, :Tq], w_[:nk, :Tq], AF.Exp, [wk, "bias"], [ak], bias=bias_sb[:nk, 0:1])
                if mask is not None:
                    tt(P, "pool", a_[:nk, :Tq], a_[:nk, :Tq], mask, AL.mult, [ak, "msk"], [ak])
                avmm(a_, ak, first, last)
                if not last:
                    mm(P, psc[:, :Tq], ones[:nk, :], sp_[:nk, :Tq], first, True, [spk, "cst"], [psck])

            with contextlib.ExitStack() as es3:
                def sb3(name, shape, dt):
                    return es3.enter_context(nc.sbuf_tensor("l_" + name, list(shape), dt))
                hn = [sb3("hn%d" % i, [128, 8, 512], BF) for i in range(2)]
                kst = sb3("kst", [128, 512], F32)
                vst = sb3("vst", [128, 128], F32)
                kT_all = sb3("kT_all", [128, 8192], BF)
                v_all = sb3("v_all", [128, 64, 128], BF)
                qT = sb3("qT", [128, 512], BF)
                gq_sb = sb3("gq", [64, 512], BF)
                gk_sb = sb3("gk", [64, 512], BF)
                sgr_sb = sb3("sgr", [128, 512], BF)
                glr_sb = sb3("glr", [16, 512], BF)
                gkv = sb3("gkv", [128, 4, 192], BF)
                ogl_st = sb3("ogl", [128, 512], BF)
                u_sb = sb3("u", [128, 64], F32)
                spk_sb = sb3("spk", [128, 64], BF)
                eb = sb3("eb", [64, 128], F32)
                enb = sb3("enb", [64, 128], F32)
                er = sb3("er", [128, 64], F32)
                qtl = sb3("qtl", [64, 128], BF)
                ktl = sb3("ktl", [64, 128], BF)
                khat = sb3("khat", [128, 64], BF)
                attm = sb3("attm", [128, 128], BF)
                o32 = sb3("o32", [128, 128], F32)
                osq = sb3("osq", [128, 128], BF)
                grs = sb3("grs", [128, 128], F32)
                Sp32 = sb3("Sp32", [64, 128], F32)
                Spbf = sb3("Spbf", [64, 128], BF)
                Ss32 = sb3("Ss32", [64, 128], F32)
                Ssbf = sb3("Ssbf", [64, 128], BF)
                P.op("dve", lambda e: e.memset(Sp32[:], 0.0), [], ["Sp32"])
                P.op("dve", lambda e: e.memset(Spbf[:], 0.0), [], ["Spbf"])
                G = PS[7]

                def gla_chunk(T, c0, j, S32, Sbf, s32k, sbfk):
                    mm(P, G[:T, 0:64], glr_sb[:16, c0:c0 + T], w2_sb[:16, :], True, True, ["glr", "w2"], ["ps7"])
                    tt(P, "dve", u_sb[:T, :], G[:T, 0:64], bg_bc[:T, :], AL.add, ["ps7", "bg"], ["u"])
                    act(P, u_sb[:T, :], u_sb[:T, :], AF.Exp, ["u"], ["u"], scale=-1.0)
                    act(P, spk_sb[:T, :], u_sb[:T, :], AF.Ln, ["u"], ["spk"], bias=1.0)
                    mm(P, G[:64, 64:64 + T], spk_sb[:T, :], tri[:T, :T], True, True, ["spk", "cst"], ["ps7"])
                    mm(P, G[:T, 192:256], su[:T, :T], spk_sb[:T, :], True, True, ["spk", "cst"], ["ps7"])
                    act(P, eb[:, :T], G[:64, 64:64 + T], AF.Exp, ["ps7"], ["eb"], scale=-1.0 / 16)
                    act(P, enb[:, :T], G[:64, 64:64 + T], AF.Exp, ["ps7"], ["enb"], scale=1.0 / 16)
                    act(P, er[:T, :], G[:T, 192:256], AF.Exp, ["ps7"], ["er"], scale=-1.0 / 16)
                    stt(P, "dve", qtl[:, :T], gq_sb[:, c0:c0 + T], 0.125, eb[:, :T], AL.mult, AL.mult, ["gq", "eb"], ["qtl"])
                    tt(P, "dve", ktl[:, :T], gk_sb[:, c0:c0 + T], enb[:, :T], AL.mult, ["gk", "enb"], ["ktl"])
                    tt(P, "dve", khat[:T, :], gkv[:T, j, 0:64], er[:T, :], AL.mult, ["gkv", "er"], ["khat"])
                    mm(P, G[:T, 256:256 + T], ktl[:, :T], qtl[:, :T], True, True, ["ktl", "qtl"], ["ps7"])
                    tt(P, "dve", attm[:T, :T], G[:T, 256:256 + T], tri[:T, :T], AL.mult, ["ps7", "cst"], ["attm"])
                    mm(P, G[:, 384:384 + T], gkv[:T, j, 64:192], attm[:T, :T], True, False, ["gkv", "attm"], ["ps7"])
                    mm(P, G[:, 384:384 + T], Sbf[:, :], qtl[:, :T], False, True, [sbfk, "qtl"], ["ps7"])
                    mm(P, PS[1][:64, 320:448], khat[:T, :], gkv[:T, j, 64:192], True, True, ["khat", "gkv"], ["ps1"])
                    stt(P, "dve", S32[:, :], S32[:, :], eb[:, T - 1:T], PS[1][:64, 320:448], AL.mult, AL.add,
                        [s32k, "eb", "ps1"], [s32k])
                    cp(P, "dve", Sbf[:, :], S32[:, :], [s32k], [sbfk])
                    cp(P, "act", o32[:, :T], G[:, 384:384 + T], ["ps7"], ["o32"])
                    tt(P, "dve", osq[:, :T], o32[:, :T], o32[:, :T], AL.mult, ["o32"], ["osq"])
                    mm(P, G[:, 64:64 + T], ones, osq[:, :T], True, True, ["osq", "cst"], ["ps7"])
                    act(P, grs[:, :T], G[:, 64:64 + T], AF.Ln, ["ps7"], ["grs"], bias=EPS, scale=1.0 / 128)
                    act(P, grs[:, :T], grs[:, :T], AF.Exp, ["grs"], ["grs"], scale=-0.5)
                    stt(P, "dve", o32[:, :T], o32[:, :T], gng[:, 0:1], grs[:, :T], AL.mult, AL.mult,
                        ["o32", "gng", "grs"], ["o32"])
                    tt(P, "dve", ogl_st[:, c0:c0 + T], o32[:, :T], sgr_sb[:, c0:c0 + T], AL.mult, ["o32", "sgr"], ["ogl"])

                it = 0
                for r in range(4):
                    for ti, (t0, w) in enumerate(TILES):
                        if it >= int(ENV.get('NTILES', '99')):
                            continue
                        par = it % 2
                        it += 1
                        h = hn[par]
                        hk = ("hn", par)
                        prompt = ti < 4
                        for c in range(8):
                            dma(P, "sp", h[:, c, :w], D["g_hn%d" % c][r * 128:(r + 1) * 128, t0:t0 + w],
                                ["g_hn%d" % c], [hk])
                        col0 = 2048 * r + t0 if prompt else 8192 + 32 * r
                        gt = 4 * r + ti
                        F = PS[0]

                        def fm(c_lo, c_hi, M):
                            for c in range(8):
                                mm(P, F[:M, :w], w0_sb[:, c, c_lo:c_hi], h[:, c, :w], c == 0, c == 7, ["w0", hk], ["ps0"])
                        fm(512, 640, 128)
                        cp(P, "act", kst[:, :w], F[:, :w], ["ps0"], ["kst"])
                        if prompt:
                            cp(P, "dve", kT_all[:, col0:col0 + w], F[:, :w], ["ps0"], [("kT", gt)])
                        else:
                            cp(P, "dve", kTs[:, 32 * r:32 * r + 32], F[:, :w], ["ps0"], ["kTs"])
                        dma(P, "sp", D["ko"][:, col0:col0 + w], kst[:, :w], ["kst"], ["ko"])
                        fm(384, 512, 128)
                        if prompt:
                            cp(P, "act", qT[:, :w], F[:, :w], ["ps0"], ["qT"])
                        else:
                            cp(P, "act", qTs[:, 32 * r:32 * r + 32], F[:, :w], ["ps0"], ["qTs"])
                        fm(0, 64, 64)
                        cp(P, "act", gq_sb[:, :w], F[:64, :w], ["ps0"], ["gq"])
                        fm(64, 128, 64)
                        cp(P, "dve", gk_sb[:, :w], F[:64, :w], ["ps0"], ["gk"])
                        fm(640, 768, 128)
                        act(P, sgr_sb[:, :w], F[:, :w], AF.Silu, ["ps0"], ["sgr"])
                        fm(768, 784, 16)
                        cp(P, "dve", glr_sb[:, :w], F[:16, :w], ["ps0"], ["glr"])
                        nb = 4
                        T = 128 if prompt else 8
                        for j in range(nb):
                            for c in range(8):
                                mm(P, PS[1][:T, 0:320], h[:, c, j * T:(j + 1) * T], w0_sb[:, c, 64:384],
                                   c == 0, c == 7, ["w0", hk], ["ps1"])
                            cp(P, "act", gkv[:T, j, :], PS[1][:T, 0:192], ["ps1"], ["gkv"])
                            if prompt:
                                cp(P, "dve", v_all[:, 4 * gt + j, :], PS[1][:T, 192:320], ["ps1"], [("v", gt)])
                            else:
                                cp(P, "dve", vs[:T, 4 * r + j, :], PS[1][:T, 192:320], ["ps1"], ["vs"])
                            cp(P, "act", vst[:T, :], PS[1][:T, 192:320], ["ps1"], ["vst"])
                            dma(P, "sp", D["vo"][col0 + j * T:col0 + (j + 1) * T, :], vst[:T, :], ["vst"], ["vo"])
                        P.begin()
                        for j in range(nb if not ENV.get('NOGLA') else 0):
                            if prompt:
                                gla_chunk(128, j * 128, j, Sp32, Spbf, "Sp32", "Spbf")
                            else:
                                sq_i = 4 * r + j
                                dma(P, "sp", Ss32[:], D["sgla"][sq_i], [], ["Ss32"])
                                cp(P, "dve", Ssbf[:], Ss32[:], ["Ss32"], ["Ssbf"])
                                gla_chunk(8, j * 8, j, Ss32, Ssbf, "Ss32", "Ssbf")
                                dma(P, "sp", D["glao"][1 + sq_i], Ss32[:], ["Ss32"], ["glao"])
                        dma(P, "sp", D["b_mx"][(2 * r) * 128:(2 * r + 1) * 128, t0:t0 + w], ogl_st[:, :w], ["ogl"], ["b_mx%d" % (2 * r)])
                        dma(P, "sp", D["dbg"][(2 * r) * 128:(2 * r + 1) * 128, t0:t0 + w], ogl_st[:, :w], ["ogl"], ["dbg"])
                        if prompt and gt == 15:
                            dma(P, "sp", D["glao"][0], Sp32[:], ["Sp32"], ["glao"])
                        st_gla = P.end()
                        P.begin()
                        if prompt and not ENV.get('NOSB'):
                            jb_hi = col0 // 128 + 3
                            nst = jb_hi + 1
                            for si in range(nst):
                                jb = jb_hi - si
                                di = jb - col0 // 128
                                mask = msk[:, di * 512:(di + 1) * 512] if di >= 0 else None

                                def zmm(psz, pszk, jb=jb):
                                    mm(P, psz[:, :512], kT_all[:, jb * 128:(jb + 1) * 128], qT[:, :512], True, True,
                                       [("kT", jb // 4), "qT"], [pszk])

                                def avmm(a_, ak, first, last, jb=jb):
                                    mm(P, PS[6][:, :512], v_all[:, jb, :], a_[:, :512], first, last, [("v", jb // 4), ak], ["ps6"])
                                sb_step(si, si == 0, si == nst - 1, 128, 512, zmm, avmm, mask)
                            cp(P, "act", o_st[:, :512], PS[6][:, :512], ["ps6"], ["o_st"])
                            dma(P, "sp", D["b_mx"][(2 * r + 1) * 128:(2 * r + 2) * 128, t0:t0 + 512], o_st[:, :512],
                                ["o_st"], ["b_mx%d" % (2 * r + 1)])
                            dma(P, "sp", D["dbg"][(2 * r + 1) * 128:(2 * r + 2) * 128, t0:t0 + 512], o_st[:, :512],
                                ["o_st"], ["dbg"])
                        st_sb = P.end()
                        P.run_streams([("gla", st_gla), ("sb", st_sb)], 2)
                P.emit()

            with contextlib.ExitStack() as es4:
              if not ENV.get('NOSAMPLE'):
                  def sb4(name, shape, dt):
                      return es4.enter_context(nc.sbuf_tensor("s_" + name, list(shape), dt))
                  kvp = [sb4("kvp%d" % i, [128, 16, 256], F32) for i in range(2)]
                  kTp = [sb4("kTp%d" % i, [128, 16, 128], BF) for i in range(2)]
                  vbf = [sb4("vbf%d" % i, [128, 16, 128], BF) for i in range(2)]
                  identf = cstf[:, 384:512]
                  NPG = int(ENV.get("NPG", "128"))

                  def zmm0(psz, pszk):
                      for q in range(16):
                          mm(P, psz[:8, 8 * q:8 * q + 8], kTs[:, 8 * q:8 * q + 8], qTs[:, 8 * q:8 * q + 8], True, True,
                             ["kTs", "qTs"], [pszk])

                  def avmm0(a_, ak, first, last):
                      for q in range(16):
                          mm(P, PS[6][:, 8 * q:8 * q + 8], vs[:8, q, :], a_[:8, 8 * q:8 * q + 8], True, NPG == 0, ["vs", ak], ["ps6"])
                  sb_step(0, True, NPG == 0, 8, 128, zmm0, avmm0, msk[:8, 2048:2176])
                  sstreams = []
                  for si in range(1, NPG + 1):
                      pg = 128 - si
                      par = si % 2
                      P.begin()
                      kv, kt, vb = kvp[par], kTp[par], vbf[par]
                      for q in range(16):
                          col = q * 128 + pg
                          P.dma("pool", lambda e, q=q, col=col, kv=kv: e.indirect_dma_start(
                              out=kv[:, q, :], out_offset=None, in_=D["pkv"],
                              in_offset=bass.IndirectOffsetOnAxis(ap=pt_sb[:, col:col + 1], axis=0)),
                              ["pt"], [("kvp", par)])
                      cp(P, "act", vb[:], kv[:, :, 128:256], [("kvp", par)], [("vbf", par)])
                      for q4 in range(4):
                          T_ = PS[par]
                          tk = "ps%d" % par
                          for qq in range(4):
                              q = 4 * q4 + qq
                              P.op("pe", lambda e, q=q, qq=qq, T_=T_, kv=kv: e.transpose(T_[:, qq * 128:(qq + 1) * 128], kv[:, q, 0:128], identf),
                                   [("kvp", par), "cstf"], [tk])
                          cp(P, "act" if q4 % 2 else "dve", kt[:, 4 * q4:4 * q4 + 4, :],
                             T_[:, :].rearrange("p (q k) -> p q k", k=128), [tk], [("kTp", par)])

                      def zmm(psz, pszk, kt=kt, par=par):
                          for q in range(16):
                              mm(P, psz[:, 8 * q:8 * q + 8], kt[:, q, :], qTs[:, 8 * q:8 * q + 8], True, True,
                                 [("kTp", par), "qTs"], [pszk])

                      def avmm(a_, ak, first, last, vb=vb, par=par):
                          for q in range(16):
                              mm(P, PS[6][:, 8 * q:8 * q + 8], vb[:, q, :], a_[:, 8 * q:8 * q + 8], False, last,
                                 [("vbf", par), ak], ["ps6"])
                      sb_step(si, False, si == NPG, 128, 128, zmm, avmm, None, sbank=(4 if par == 0 else 7),
                              after=("s%d" % (si - 1)))
                      sstreams.append(("s%d" % si, P.end()))
                  P.run_streams(sstreams, 2)
                  cp(P, "act", o_st[:, :128], PS[6][:, :128], ["ps6"], ["o_st"])
                  for r in range(4):
                      dma(P, "sp", D["b_mx"][(2 * r + 1) * 128:(2 * r + 2) * 128, 2048:2080], o_st[:, 32 * r:32 * r + 32],
                          ["o_st"], ["b_mx%d" % (2 * r + 1)])
                      dma(P, "sp", D["dbg"][(2 * r + 1) * 128:(2 * r + 2) * 128, 2048:2080], o_st[:, 32 * r:32 * r + 32],
                          ["o_st"], ["dbg"])
                  P.emit()
        ridx_sb = sb("ridx", [128, 24], I32)
        dma(P, "sp", ridx_sb[:], D["ridx"], [], ["ridx"])

        def post_norm_add(ti, src, skey, nidx, tmp):
            t0, w = TILES[ti]
            sq, rstd, t32 = tmp
            for c in range(8):
                act(P, sq[:, c, :w], src[:, c, :w], AF.Square, [skey], ["sq"])
            for c in range(8):
                mm(P, PS[0][:, :w], ones, sq[:, c, :w], c == 0, c == 7, ["sq", "cst"], ["ps0"])
            act(P, rstd[:, :w], PS[0][:, :w], AF.Ln, ["ps0"], ["rstd"], bias=EPS, scale=1.0 / 1024)
            act(P, rstd[:, :w], rstd[:, :w], AF.Exp, ["rstd"], ["rstd"], scale=-0.5)
            for c in range(8):
                stt(P, "dve", t32[:, :w], src[:, c, :w], nrm[:, nidx * 8 + c:nidx * 8 + c + 1], rstd[:, :w],
                    AL.mult, AL.mult, [skey, "rstd", "nrm"], ["t32"])
                tt(P, "pool", x_sb[:, c, t0:t0 + w], x_sb[:, c, t0:t0 + w], t32[:, :w], AL.add, [("x", c), "t32"], [("x", c)])

        def out_proj(layer):
            nk = 8 if layer == 0 else 16
            wname = "wout0" if layer == 0 else "wout1"
            gname = "g_mx" if layer == 0 else "g_y1"
            with contextlib.ExitStack() as esA:
                def sbA(name, shape, dt):
                    return esA.enter_context(nc.sbuf_tensor("A%d_" % layer + name, list(shape), dt))
                wo = sbA("wo", [128, nk, 1024], BF)
                mxf = sbA("mxf", [128, nk, NT], BF)
                m_sb = sbA("m", [128, 8, 512], F32)
                sq = sbA("sq", [128, 8, 512], BF)
                rstd = sbA("rstd", [128, 512], F32)
                t32 = sbA("t32", [128, 512], F32)
                for c in range(nk):
                    dma(P, "pool", wo[:, c, :], D[wname][c * 128:(c + 1) * 128, :], [], ["wo"])
                for c in range(nk):
                    col = c if layer == 0 else 8 + c
                    P.dma("pool", lambda e, c=c, col=col: e.indirect_dma_start(
                        out=mxf[:, c, :], out_offset=None, in_=D[gname],
                        in_offset=bass.IndirectOffsetOnAxis(ap=ridx_sb[:, col:col + 1], axis=0)),
                        ["ridx", gname], [("mx", c)])
                for ti, (t0, w) in enumerate(TILES):
                    for oc in range(8):
                        pso = PS[1 + oc % 2]
                        pk = "ps%d" % (1 + oc % 2)
                        for c in range(nk):
                            mm(P, pso[:, :w], wo[:, c, oc * 128:(oc + 1) * 128], mxf[:, c, t0:t0 + w], c == 0, c == nk - 1,
                               ["wo", ("mx", c)], [pk])
                        cp(P, "act", m_sb[:, oc, :w], pso[:, :w], [pk], ["m_sb"])
                    post_norm_add(ti, m_sb, "m_sb", 4 * layer + 1, (sq, rstd, t32))
                P.emit()

        def mlp(layer):
            with contextlib.ExitStack() as esB:
                def sbB(name, shape, dt):
                    return esB.enter_context(nc.sbuf_tensor("B%d_" % layer + name, list(shape), dt))
                W = 544
                hTt = sbB("hT", [128, 8, W], BF)
                hid = sbB("hid", [128, 32, W], BF)
                f_sb = sbB("f", [128, 8, W], F32)
                sq = sbB("sq", [128, 8, 512], BF)
                rstd = sbB("rstd", [128, 512], F32)
                t32 = sbB("t32", [128, 512], F32)
                r32 = [sbB("r32_%d" % i, [128, 512], F32) for i in range(2)]
                wu = [sbB("wu%d" % i, [128, 8, 512], BF) for i in range(2)]
                wd = [sbB("wd%d" % i, [128, 4, 1024], BF) for i in range(2)]
                for st in STS:
                    offs = {}
                    o = 0
                    for ti in st:
                        offs[ti] = o
                        o += TILES[ti][1]
                    for ti in st:
                        w = TILES[ti][1]
                        rmsnorm_tile(ti, 4 * layer + 2, lambda c, o=offs[ti], w=w: hTt[:, c, o:o + w], ["hT"], (sq, rstd, PS[0], "ps0"))
                    for pc in range(8):
                        wt, wk = wu[pc % 2], ("wu", pc % 2)
                        dma(P, "pool", wt[:], D["wup"][layer].rearrange("(c p) n -> p c n", p=128)[:, :, pc * 512:(pc + 1) * 512],
                            [], [wk])
                        for ti in st:
                            w, o = TILES[ti][1], offs[ti]
                            for f4 in range(4):
                                f = pc * 4 + f4
                                pu = PS[1 + f % 2]
                                pk = "ps%d" % (1 + f % 2)
                                for c in range(8):
                                    mm(P, pu[:, :w], wt[:, c, f4 * 128:(f4 + 1) * 128], hTt[:, c, o:o + w], c == 0, c == 7,
                                       [wk, "hT"], [pk])
                                rt, rk = r32[f % 2], ("r32", f % 2)
                                act(P, rt[:, :w], pu[:, :w], AF.Relu, [pk], [rk])
                                tt(P, "dve", hid[:, f, o:o + w], rt[:, :w], rt[:, :w], AL.mult, [rk], ["hid"])
                    for half in range(2):
                        for ti in st:
                            w, o = TILES[ti][1], offs[ti]
                            for pc in range(8):
                                wt, wk = wd[pc % 2], ("wd", pc % 2)
                                if True:
                                    dma(P, "pool", wt[:, :, 0:512],
                                        D["wdn"][layer][pc * 512:(pc + 1) * 512, half * 512:(half + 1) * 512].rearrange("(c p) n -> p c n", p=128),
                                        [], [wk])
                                for oc in range(4):
                                    for f4 in range(4):
                                        mm(P, PS[3 + oc][:, :w], wt[:, f4, oc * 128:(oc + 1) * 128], hid[:, pc * 4 + f4, o:o + w],
                                           pc == 0 and f4 == 0, pc == 7 and f4 == 3, [wk, "hid"], ["ps%d" % (3 + oc)])
                            for oc in range(4):
                                cp(P, "act", f_sb[:, half * 4 + oc, o:o + w], PS[3 + oc][:, :w], ["ps%d" % (3 + oc)], ["f_sb"])
                    for ti in st:
                        w, o = TILES[ti][1], offs[ti]
                        post_norm_add(ti, f_sb[:, :, o:o + w], "f_sb", 4 * layer + 3, (sq, rstd, t32))
                P.emit()

        for sl in range(8):
            allgather(D["b_mx"][sl * 128:(sl + 1) * 128, :], D["g_mx"][sl * 512:(sl + 1) * 512, :], "b_mx%d" % sl, "g_mx")
        out_proj(0)
        if not ENV.get('NOMLP'):
            mlp(0)
        def ssd_layer():
            prenorm_gather(4, "p1")
            with contextlib.ExitStack() as es5:
                def sb5(name, shape, dt):
                    return es5.enter_context(nc.sbuf_tensor("d_" + name, list(shape), dt))
                trif, suf, onesf = cstf[:, 0:128], cstf[:, 128:256], cstf[:, 256:384]
                w1_sb = sb5("w1", [128, 8, 1544], BF)
                convw = sb5("convw", [128, 32], F32)
                convb = sb5("convb", [128, 8], F32)
                dtb = sb5("dtb", [128, 8], F32)
                a_bc = sb5("abc", [128, 8], F32)
                dsk = sb5("dsk", [128, 512], F32)
                sng = sb5("sng", [128, 512], F32)
                hn = [sb5("hn%d" % i, [128, 8, 512], BF) for i in range(2)]
                xbc = sb5("xbc", [128, 8, 515], F32)
                hal = sb5("hal", [128, 8, 3], F32)
                acc = [sb5("acc%d" % i, [128, 512], F32) for i in range(2)]
                cvo = sb5("cvo", [128, 8, 512], BF)
                zs = sb5("zs", [128, 4, 512], BF)
                dt_sb = sb5("dt", [128, 4, 8], F32)
                dA_sb = sb5("dA", [128, 4, 8], F32)
                x_tok = sb5("xtok", [128, 4, 512], BF)
                B_tok = sb5("btok", [128, 4, 256], BF)
                expcum = sb5("expcum", [128, 8], F32)
                declast = sb5("declast", [128, 8], F32)
                R = sb5("R", [128, 4, 128], F32)
                Gm = sb5("Gm", [128, 4, 128], F32)
                cbm = sb5("cbm", [128, 128], F32)
                att = sb5("att", [128, 4, 128], BF)
                ysum = sb5("ysum", [128, 512], F32)
                ytmp = sb5("ytmp", [128, 512], F32)
                ss = sb5("ss", [128, 2], F32)
                yn = sb5("yn", [128, 512], BF)
                coef = sb5("coef", [128, 8], F32)
                xs = sb5("xs", [128, 512], BF)
                hp32 = sb5("hp32", [128, 512], F32)
                hpbf = sb5("hpbf", [128, 512], BF)
                hs32 = sb5("hs32", [128, 512], F32)
                hsbf = sb5("hsbf", [128, 512], BF)
                y_st = sb5("y_st", [128, 4, 512], BF)
                dma(P, "pool", w1_sb[:], D["w1"].rearrange("(c p) n -> p c n", p=128), [], ["w1"])
                dma(P, "sp", convw[:], D["convw"], [], ["convw"])
                dma(P, "sp", convb[:], D["convb"], [], ["convb"])
                dma(P, "sp", dtb[:], D["dtb"].partition_broadcast(128), [], ["dtb"])
                dma(P, "sp", a_bc[:], D["alog"].partition_broadcast(128), [], ["abc"])
                dma(P, "sp", dsk[:], D["dsk"].partition_broadcast(128), [], ["dsk"])
                dma(P, "sp", sng[:], D["sng"].partition_broadcast(128), [], ["sng"])
                act(P, a_bc[:], a_bc[:], AF.Exp, ["abc"], ["abc"])
                ts(P, "dve", a_bc[:], a_bc[:], -1.0, None, AL.mult, None, ["abc"], ["abc"])
                P.op("dve", lambda e: e.memset(hal[:], 0.0), [], ["hal"])
                P.op("dve", lambda e: e.memset(hp32[:], 0.0), [], ["hp32"])
                P.op("dve", lambda e: e.memset(hpbf[:], 0.0), [], ["hpbf"])

                def conv_block(T, src_fn, out_c0):
                    for ch in range(8):
                        eng = "dve"
                        ac, ak = acc[ch % 2], ("acc", ch % 2)
                        src = src_fn(ch)
                        ts(P, eng, ac[:, :T], src[:, 0:T], convw[:, ch * 4:ch * 4 + 1], None, AL.mult, None, ["xbc", "convw"], [ak])
                        for w_ in range(1, 4):
                            stt(P, eng, ac[:, :T], src[:, w_:w_ + T], convw[:, ch * 4 + w_:ch * 4 + w_ + 1], ac[:, :T],
                                AL.mult, AL.add, ["xbc", "convw", ak], [ak])
                        act(P, cvo[:, ch, out_c0:out_c0 + T], ac[:, :T], AF.Silu, [ak, "convb"], ["cvo"], bias=convb[:, ch:ch + 1])

                xh = sb5("xh", [128, 512], BF)

                def bc_last(ap2d, T, H, W):
                    return ap2d.rearrange("p (h o) -> p h o", o=1).to_broadcast([T, H, W])

                def bc_mid(ap2d, T, H, W):
                    return ap2d.rearrange("p (o t) -> p o t", o=1).to_broadcast([T, H, W])

                def v3(ap2d, H):
                    return ap2d.rearrange("p (h c) -> p h c", h=H)

                def ssd_chunk(T, c0, j, h32, hbf, h32k, hbfk):
                    dtj, dAj = dt_sb[:T, j, :], dA_sb[:T, j, :]
                    mm(P, PS[3][:T, 0:8], trif[:T, :T], dAj, True, True, ["dA", "cstf"], ["ps3"])
                    act(P, expcum[:T, :], PS[3][:T, 0:8], AF.Exp, ["ps3"], ["expcum"])
                    mm(P, PS[3][:, 8:16], onesf[:T, :], dAj, True, True, ["dA", "cstf"], ["ps3"])
                    act(P, declast[:, :], PS[3][:, 8:16], AF.Exp, ["ps3"], ["declast"])
                    tt(P, "pool", v3(xh[:T, :], 8), v3(x_tok[:T, j, :], 8), bc_last(dtj, T, 8, 64), AL.mult, ["xtok", "dt"], ["xh"])
                    mm(P, PS[6][:T, 0:256], cvo[:, 6, c0:c0 + T], hbf[:, 0:256], True, True, ["cvo", hbfk], ["ps6"])
                    mm(P, PS[6][:T, 256:512], cvo[:, 7, c0:c0 + T], hbf[:, 256:512], True, True, ["cvo", hbfk], ["ps6"])
                    P4 = PS[4][:, :].rearrange("p (h t) -> p h t", h=4)
                    for g in range(2):
                        tt(P, "pool", R[:T, :, :T], bc_mid(trif[:T, :T], T, 4, T), bc_last(dA_sb[:T, j, g * 4:(g + 1) * 4], T, 4, T),
                           AL.mult, ["dA", "cstf"], ["R"])
                        if T == 128:
                            mm(P, PS[4][:, :], suf, R[:, :, :].rearrange("p h t -> p (h t)"), True, True, ["R", "cstf"], ["ps4"])
                        else:
                            for hh in range(4):
                                mm(P, PS[4][:T, hh * 128:hh * 128 + T], suf[:T, :T], R[:T, hh, :T], True, True, ["R", "cstf"], ["ps4"])
                        act(P, Gm[:T, :, :T], P4[:T, :, :T], AF.Exp, ["ps4"], ["Gm"])
                        mm(P, PS[3][:T, 16:16 + T], cvo[:, 4 + g, c0:c0 + T], cvo[:, 6 + g, c0:c0 + T], True, True, ["cvo"], ["ps3"])
                        tt(P, "dve", cbm[:T, :T], PS[3][:T, 16:16 + T], trif[:T, :T], AL.mult, ["ps3", "cstf"], ["cbm"])
                        tt(P, "dve", att[:T, :, :T], Gm[:T, :, :T], bc_mid(cbm[:T, :T], T, 4, T), AL.mult, ["Gm", "cbm"], ["att"])
                        for hh in range(4):
                            h = g * 4 + hh
                            mm(P, PS[5][:T, h * 64:(h + 1) * 64], att[:T, hh, :T], xh[:T, h * 64:(h + 1) * 64], True, True,
                               ["att", "xh"], ["ps5"])
                        cp(P, "dve", coef[:T, g * 4:(g + 1) * 4], Gm[:T, :, T - 1], ["Gm"], ["coef"])
                    tt(P, "dve", v3(ytmp[:T, :], 8), v3(PS[6][:T, :], 8), bc_last(expcum[:T, :], T, 8, 64), AL.mult,
                       ["ps6", "expcum"], ["ytmp"])
                    tt(P, "dve", ysum[:T, :], ytmp[:T, :], PS[5][:T, :], AL.add, ["ytmp", "ps5"], ["ysum"])
                    tt(P, "pool", ytmp[:T, :], x_tok[:T, j, :], dsk[:T, :], AL.mult, ["xtok", "dsk"], ["ytmp"])
                    tt(P, "pool", ysum[:T, :], ysum[:T, :], ytmp[:T, :], AL.add, ["ysum", "ytmp"], ["ysum"])
                    tt(P, "pool", ysum[:T, :], ysum[:T, :], zs[:T, j, :], AL.mult, ["ysum", "zs"], ["ysum"])
                    tt(P, "dve", ytmp[:T, :], ysum[:T, :], ysum[:T, :], AL.mult, ["ysum"], ["ytmp"])
                    P.op("dve", lambda e: e.tensor_reduce(ss[:T, 0:2], ytmp[:T, :].rearrange("p (g c) -> p g c", g=2),
                                                          mybir.AxisListType.X, AL.add), ["ytmp"], ["ss"])
                    act(P, ss[:T, :], ss[:T, :], AF.Ln, ["ss"], ["ss"], bias=EPS, scale=1.0 / 256)
                    act(P, ss[:T, :], ss[:T, :], AF.Exp, ["ss"], ["ss"], scale=-0.5)
                    tt(P, "dve", v3(ytmp[:T, :], 2), v3(ysum[:T, :], 2), bc_last(ss[:T, :], T, 2, 256), AL.mult, ["ysum", "ss"], ["ytmp"])
                    tt(P, "pool", yn[:T, :], ytmp[:T, :], sng[:T, :], AL.mult, ["ytmp", "sng"], ["yn"])
                    for ch in range(4):
                        mm(P, PS[2][:, ch * 128:ch * 128 + T], yn[:T, ch * 128:(ch + 1) * 128], ident[:T, :T], True, True,
                           ["yn", "cst"], ["ps2"])
                    cp(P, "act", y_st[:, :, c0:c0 + T], PS[2][:, :].rearrange("p (c t) -> p c t", c=4)[:, :, :T], ["ps2"], ["y_st"])
                    tt(P, "pool", v3(xs[:T, :], 8), v3(xh[:T, :], 8), bc_last(coef[:T, :], T, 8, 64), AL.mult, ["xh", "coef"], ["xs"])
                    for g in range(2):
                        mm(P, PS[7][:, g * 256:(g + 1) * 256], B_tok[:T, j, g * 128:(g + 1) * 128], xs[:T, g * 256:(g + 1) * 256],
                           True, True, ["btok", "xs"], ["ps7"])
                    tt(P, "dve", v3(h32[:, :], 8), v3(h32[:, :], 8), bc_last(declast[:, :], 128, 8, 64), AL.mult, [h32k, "declast"], [h32k])
                    tt(P, "dve", h32[:, :], h32[:, :], PS[7][:, :], AL.add, [h32k, "ps7"], [h32k])
                    cp(P, "act", hbf[:, :], h32[:, :], [h32k], [hbfk])

                it = 0
                for r in range(4):
                    for ti, (t0, w) in enumerate(TILES):
                        if it >= int(ENV.get('NTILES1', '99')):
                            continue
                        par = it % 2
                        it += 1
                        h = hn[par]
                        hk = ("hn", par)
                        prompt = ti < 4
                        gt = 4 * r + ti
                        for c in range(8):
                            dma(P, "sp", h[:, c, :w], D["g_hn%d" % c][r * 128:(r + 1) * 128, t0:t0 + w], ["g_hn%d" % c], [hk])
                        if prompt:
                            for ch in range(8):
                                cp(P, "pool", xbc[:, ch, 0:3], hal[:, ch, :], ["hal"], ["xbc"])
                        for ch in range(8):
                            for c in range(8):
                                mm(P, PS[0][:, :w], w1_sb[:, c, 512 + ch * 128:512 + (ch + 1) * 128], h[:, c, :w], c == 0, c == 7,
                                   ["w1", hk], ["ps0"])
                            if prompt:
                                cp(P, "act", xbc[:, ch, 3:3 + w], PS[0][:, :w], ["ps0"], ["xbc"])
                            else:
                                for b_ in range(4):
                                    cp(P, "act", xbc[:, ch, b_ * 11 + 3:b_ * 11 + 11], PS[0][:, b_ * 8:b_ * 8 + 8], ["ps0"], ["xbc"])
                        if prompt:
                            for ch in range(8):
                                cp(P, "pool", hal[:, ch, :], xbc[:, ch, 512:515], ["xbc"], ["hal"])
                            if gt == 15:
                                dma(P, "sp", D["convo"][0].rearrange("p (c t) -> p c t", t=3), xbc[:, :, 512:515], ["xbc"], ["convo"])
                            conv_block(512, lambda ch: xbc[:, ch, :], 0)
                        else:
                            for b_ in range(4):
                                sq_i = 4 * r + b_
                                dma(P, "sp", xbc[:, :, b_ * 11:b_ * 11 + 3], D["sconv"][sq_i].rearrange("p (c t) -> p c t", t=3),
                                    [], ["xbc"])
                            for b_ in range(4):
                                sq_i = 4 * r + b_
                                dma(P, "sp", D["convo"][1 + sq_i].rearrange("p (c t) -> p c t", t=3), xbc[:, :, b_ * 11 + 8:b_ * 11 + 11],
                                    ["xbc"], ["convo"])
                                conv_block(8, lambda ch, b_=b_: xbc[:, ch, b_ * 11:b_ * 11 + 11], b_ * 8)
                        T = 128 if prompt else 8
                        for j in range(4):
                            for c in range(8):
                                mm(P, PS[1][:T, :], h[:, c, j * T:(j + 1) * T], w1_sb[:, c, 0:512], c == 0, c == 7, ["w1", hk], ["ps1"])
                            act(P, zs[:T, j, :], PS[1][:T, :], AF.Silu, ["ps1"], ["zs"])
                            for c in range(8):
                                mm(P, PS[3][:T, 0:8], h[:, c, j * T:(j + 1) * T], w1_sb[:, c, 1536:1544], c == 0, c == 7, ["w1", hk], ["ps3"])
                            tt(P, "dve", dt_sb[:T, j, :], PS[3][:T, 0:8], dtb[:T, :], AL.add, ["ps3", "dtb"], ["dt"])
                        act(P, dt_sb[:T, :, :], dt_sb[:T, :, :], AF.Exp, ["dt"], ["dt"])
                        act(P, dt_sb[:T, :, :], dt_sb[:T, :, :], AF.Ln, ["dt"], ["dt"], bias=1.0)
                        for j in range(4):
                            tt(P, "dve", dA_sb[:T, j, :], dt_sb[:T, j, :], a_bc[:T, :], AL.mult, ["dt", "abc"], ["dA"])
                            for ch in range(4):
                                mm(P, PS[2][:T, ch * 128:(ch + 1) * 128], cvo[:, ch, j * T:(j + 1) * T], ident, True, True, ["cvo", "cst"], ["ps2"])
                            cp(P, "dve", x_tok[:T, j, :], PS[2][:T, :], ["ps2"], ["xtok"])
                            for g in range(2):
                                mm(P, PS[2][:T, g * 128:(g + 1) * 128], cvo[:, 4 + g, j * T:(j + 1) * T], ident, True, True, ["cvo", "cst"], ["ps2"])
                            cp(P, "dve", B_tok[:T, j, :], PS[2][:T, 0:256], ["ps2"], ["btok"])
                        for j in range(4 if not ENV.get('NOSSD') else 0):
                            if prompt:
                                ssd_chunk(128, j * 128, j, hp32, hpbf, "hp32", "hpbf")
                            else:
                                sq_i = 4 * r + j
                                dma(P, "sp", hs32[:], D["sssm"][sq_i], [], ["hs32"])
                                cp(P, "dve", hsbf[:], hs32[:], ["hs32"], ["hsbf"])
                                ssd_chunk(8, j * 8, j, hs32, hsbf, "hs32", "hsbf")
                                dma(P, "sp", D["ssmo"][1 + sq_i], hs32[:], ["hs32"], ["ssmo"])
                        if prompt and gt == 15:
                            dma(P, "sp", D["ssmo"][0], hp32[:], ["hp32"], ["ssmo"])
                        for ch in range(4):
                            sl = 4 * r + ch
                            dma(P, "sp", D["b_y1"][sl * 128:(sl + 1) * 128, t0:t0 + w], y_st[:, ch, :w], ["y_st"], ["b_y1_%d" % sl])
                P.emit()
            for sl in range(16):
                allgather(D["b_y1"][sl * 128:(sl + 1) * 128, :], D["g_y1"][sl * 512:(sl + 1) * 512, :], "b_y1_%d" % sl, "g_y1")
            out_proj(1)
            if not ENV.get('NOMLP'):
                mlp(1)
        if not ENV.get('NOL1'):
            ssd_layer()

        for c in range(8):
            dma(P, "sp", D["yT"][c * 128:(c + 1) * 128, :], x_sb[:, c, :], [("x", c)], ["yT"])
        P.emit()
    return nc


_NC = None


def _consts():
    j = np.arange(128)
    tri = (j[:, None] <= j[None, :]).astype(np.float32)
    su = (j[:, None] > j[None, :]).astype(np.float32)
    ones = np.ones((128, 128), np.float32)
    ident = np.eye(128, dtype=np.float32)
    cst = np.concatenate([tri, su, ones, ident], axis=1)
    t = np.arange(512)
    msk = np.zeros((128, 5 * 512), np.float32)
    for i in range(4):
        msk[:, i * 512:(i + 1) * 512] = (128 * i + j[:, None] < t[None, :])
    tt8 = np.arange(128) % 8
    msk[:8, 2048:2048 + 128] = (np.arange(8)[:, None] < tt8[None, :])
    return cst, msk


def _col(v):
    return np.ascontiguousarray(np.asarray(v, np.float32).reshape(8, 128).T)


def kernel(**inp):
    global _NC
    if _NC is None:
        _NC = build()
    nc = _NC
    f = lambda k: np.asarray(inp[k])
    xp, xs = f("x_prompt"), f("x_sample")
    w_in = f("w_in_even")[0]
    w2 = f("gla_w_gate2")[0]
    bgate = f("gla_b_gate")[0]
    wout = f("w_out_even")[0]
    ssd_w_in = f("ssd_w_in")[0]
    cst, msk = _consts()
    norms = np.concatenate([_col(f(n)[l]) for l in range(2)
                            for n in ("norm_mix_pre", "norm_mix_post", "norm_mlp_pre", "norm_mlp_post")], axis=1)
    ck, cv = f("cache_sb_k")[0], f("cache_sb_v")[0]
    pkvs = [np.ascontiguousarray(np.stack([ck[:, :, h, :], cv[:, :, h, :]], axis=2)).reshape(-1, 256) for h in range(4)]
    iota = np.arange(128, dtype=np.float32).reshape(128, 1)
    conv_w, conv_b = f("ssd_conv_w")[0], f("ssd_conv_b")[0]
    in_maps = []
    for c in range(8):
        g, k = c // 4, c % 4
        m = {}
        m["xT"] = np.ascontiguousarray(np.concatenate(
            [xp[g, 2048 * k:2048 * (k + 1)].T, xs[4 * c:4 * c + 4].reshape(32, 1024).T], axis=1))
        m["rank"] = np.array([[k, g, c, 0]], np.int32)
        cols = np.concatenate([np.arange(64 * k, 64 * k + 64), 256 + np.arange(64 * k, 64 * k + 64),
                               512 + np.arange(128 * k, 128 * k + 128),
                               2576 + np.arange(128 * k, 128 * k + 128),
                               1552 + np.arange(128 * k, 128 * k + 128),
                               2064 + np.arange(128 * k, 128 * k + 128),
                               1040 + np.arange(128 * k, 128 * k + 128),
                               1024 + np.arange(16)])
        m["w0"] = np.ascontiguousarray(w_in[:, cols])
        m["w2g"] = np.ascontiguousarray(w2[:, 64 * k:64 * k + 64])
        m["bg"] = np.ascontiguousarray(bgate[None, 64 * k:64 * k + 64])
        m["gng"] = np.ascontiguousarray(f("gla_norm_g")[0].reshape(128, 1))
        m["sbb"] = np.full((128, 1), f("sb_logit_bias")[0, k], np.float32)
        rows = np.concatenate([np.concatenate([np.arange(128 * r, 128 * r + 128),
                                               512 + np.arange(128 * r, 128 * r + 128)]) for r in range(4)])
        m["wout0"] = np.ascontiguousarray(wout[rows, :])
        m["norms"] = norms
        m["wup"] = f("mlp_w_up")
        m["wdn"] = f("mlp_w_down")
        m["sgla"] = np.ascontiguousarray(f("state_gla")[0][16 * g:16 * g + 16, k])
        m["pkv"] = pkvs[k]
        m["iota"] = iota
        ri = np.zeros((128, 24), np.int32)
        for i_ in range(4):
            for part in range(2):
                ri[:, 2 * i_ + part] = (2 * k + part) * 512 + i_ * 128 + np.arange(128)
            for q_ in range(4):
                ri[:, 8 + 4 * i_ + q_] = (4 * k + q_) * 512 + i_ * 128 + np.arange(128)
        m["ridx"] = ri
        m["pt"] = np.ascontiguousarray(f("page_table")[16 * g:16 * g + 16].reshape(1, 2048))
        m["cst"], m["msk"] = cst, msk
        c1 = np.concatenate([np.arange(512 * k, 512 * k + 512), 2048 + np.arange(512 * k, 512 * k + 512),
                             4096 + np.arange(256 * k, 256 * k + 256), 5120 + np.arange(256 * k, 256 * k + 256),
                             6144 + np.arange(8 * k, 8 * k + 8)])
        m["w1"] = np.ascontiguousarray(ssd_w_in[:, c1])
        m["wout1"] = f("ssd_w_out")[0]
        cch = np.concatenate([np.arange(512 * k, 512 * k + 512), 2048 + np.arange(256 * k, 256 * k + 256),
                              3072 + np.arange(256 * k, 256 * k + 256)])
        m["convw"] = np.ascontiguousarray(conv_w[:, cch].reshape(4, 8, 128).transpose(2, 1, 0).reshape(128, 32))
        m["convb"] = np.ascontiguousarray(conv_b[cch].reshape(8, 128).T)
        m["dtb"] = np.ascontiguousarray(f("ssd_dt_bias")[0][None, 8 * k:8 * k + 8])
        m["alog"] = np.ascontiguousarray(f("ssd_a_log")[0][None, 8 * k:8 * k + 8])
        m["dsk"] = np.ascontiguousarray(np.repeat(f("ssd_d")[0][8 * k:8 * k + 8], 64)[None, :])
        m["sng"] = np.ascontiguousarray(f("ssd_norm_g")[0][None, 512 * k:512 * k + 512])
        ss = f("state_ssm")[0][16 * g:16 * g + 16, 8 * k:8 * k + 8]
        m["sssm"] = np.ascontiguousarray(ss.transpose(0, 3, 1, 2).reshape(16, 128, 512))
        sc = f("state_conv")[0][16 * g:16 * g + 16][:, :, cch]
        m["sconv"] = np.ascontiguousarray(sc.reshape(16, 3, 8, 128).transpose(0, 3, 2, 1).reshape(16, 128, 24))
        in_maps.append({k_: v_ for k_, v_ in m.items() if k_ in USED})
    res = run_bass_kernel_spmd(nc, in_maps, core_ids=list(range(8))).results

    y_p = np.zeros((2, 8192, 1024), np.float32)
    y_s = np.zeros((32, 8, 1024), np.float32)
    k_p = np.zeros((1, 2, 8192, 4, 128), np.float32)
    v_p = np.zeros_like(k_p)
    k_s = np.zeros((1, 32, 8, 4, 128), np.float32)
    v_s = np.zeros_like(k_s)
    gla_p = np.zeros((1, 2, 4, 64, 128), np.float32)
    gla_s = np.zeros((1, 32, 4, 64, 128), np.float32)
    ssm_p = np.zeros((1, 2, 32, 64, 128), np.float32)
    ssm_s = np.zeros((1, 32, 32, 64, 128), np.float32)
    conv_p = np.zeros((1, 2, 3, 4096), np.float32)
    conv_s = np.zeros((1, 32, 3, 4096), np.float32)
    for c in range(8):
        g, k = c // 4, c % 4
        r = res[c]
        yT = np.asarray(r["yT"])
        y_p[g, 2048 * k:2048 * (k + 1)] = yT[:, :2048].T
        y_s[4 * c:4 * c + 4] = yT[:, 2048:].T.reshape(4, 8, 1024)
        ko, vo = np.asarray(r["ko"]), np.asarray(r["vo"])
        k_p[0, g, :, k, :] = ko[:, :8192].T
        v_p[0, g, :, k, :] = vo[:8192]
        k_s[0, 16 * g:16 * g + 16, :, k, :] = ko[:, 8192:].T.reshape(16, 8, 128)
        v_s[0, 16 * g:16 * g + 16, :, k, :] = vo[8192:].reshape(16, 8, 128)
        go = np.asarray(r["glao"])
        gla_p[0, g, k] = go[0]
        gla_s[0, 16 * g:16 * g + 16, k] = go[1:]
        so = np.asarray(r["ssmo"]).reshape(17, 128, 8, 64).transpose(0, 2, 3, 1)
        ssm_p[0, g, 8 * k:8 * k + 8] = so[0]
        ssm_s[0, 16 * g:16 * g + 16, 8 * k:8 * k + 8] = so[1:]
        co = np.asarray(r["convo"]).reshape(17, 128, 8, 3).transpose(0, 3, 2, 1).reshape(17, 3, 1024)
        cch = np.concatenate([np.arange(512 * k, 512 * k + 512), 2048 + np.arange(256 * k, 256 * k + 256),
                              3072 + np.arange(256 * k, 256 * k + 256)])
        conv_p[0, g][:, cch] = co[0]
        conv_s[0, 16 * g:16 * g + 16][:, :, cch] = co[1:]
    global DBG
    DBG = [np.asarray(res[c]['dbg']).astype(np.float32) for c in range(8)]
    return (y_p, y_s, k_p, v_p, gla_p, ssm_p, conv_p, k_s, v_s, gla_s, ssm_s, conv_s)
```
